# Optimizing a Trainium2 kernel written in Bass

```python
import jax, jax.numpy as jnp
from jax import lax
import numpy as np

D_MODEL = 1024
BATCH = 2
SEQ = 8192
DEPTH = 2
DEC_BATCH = 16
DEC_SEQ = 2048
PAST_LEN = 128

POOL_WIDTH = D_MODEL
N_POOL_GROUPS = 4
POOL_GROUP = POOL_WIDTH // N_POOL_GROUPS
POOL_WINDOWS = (2, 4, 8, 16)
CONV_WIDTH = D_MODEL
CONV_K = 3
IN_COLS = POOL_WIDTH + 3 * CONV_WIDTH + 2 * D_MODEL
FFN_HIDDEN = ((8 * D_MODEL // 3 + 255) // 256) * 256
EPS = 1e-6

kernel_name = "hybrid_pool_shortconv_encoder"


def rmsnorm(x, g):
    xf = x.astype(jnp.float32)
    r = xf * lax.rsqrt(jnp.mean(xf * xf, axis=-1, keepdims=True) + EPS)
    return (r * g.astype(jnp.float32)).astype(x.dtype)


def centred_mean_minus_self(g, w):
    s = g.shape[1]
    half = w // 2
    gp = jnp.pad(g.astype(jnp.float32), ((0, 0), (half, half), (0, 0)))
    cs = jnp.cumsum(gp, axis=1)
    cs = jnp.concatenate([jnp.zeros_like(cs[:, :1]), cs], axis=1)
    wsum = cs[:, w:w + s] - cs[:, :s]
    pos = jnp.arange(s, dtype=jnp.int32)
    cnt = (jnp.minimum(pos + half, s) - jnp.maximum(pos - half, 0)).astype(jnp.float32)
    mean = wsum / cnt[None, :, None]
    return (mean - g.astype(jnp.float32)).astype(g.dtype)


def multiscale_pool_branch(u_pool, pool_w, pool_scale):
    outs = []
    for i, w in enumerate(POOL_WINDOWS):
        g = u_pool[..., i * POOL_GROUP:(i + 1) * POOL_GROUP]
        d = centred_mean_minus_self(g, w)
        outs.append(jnp.einsum('bsg,gh->bsh', d, pool_w[i]))
    return jnp.concatenate(outs, axis=-1) * pool_scale


def short_conv_branch(b_gate, c_gate, v, conv_w):
    cv = c_gate * v
    s = cv.shape[1]
    cp = jnp.pad(cv, ((0, 0), (1, 1), (0, 0)))
    y = cp[:, 0:s] * conv_w[0] + cp[:, 1:s + 1] * conv_w[1] + cp[:, 2:s + 2] * conv_w[2]
    return b_gate * y


def encoder_layer(x, w_in, pool_w, pool_scale, conv_w, w_out, w_ffn_gate, w_ffn_up, w_ffn_down,
                  g_pre_mix, g_post_mix, g_pre_ffn, g_post_ffn):
    h = rmsnorm(x, g_pre_mix)
    u = jnp.einsum('bsd,dc->bsc', h, w_in)
    o = 0
    u_pool = u[..., o:o + POOL_WIDTH]; o += POOL_WIDTH
    b_gate = u[..., o:o + CONV_WIDTH]; o += CONV_WIDTH
    c_gate = u[..., o:o + CONV_WIDTH]; o += CONV_WIDTH
    v = u[..., o:o + CONV_WIDTH]; o += CONV_WIDTH
    ga = u[..., o:o + D_MODEL]; o += D_MODEL
    gb = u[..., o:o + D_MODEL]
    y_a = multiscale_pool_branch(u_pool, pool_w, pool_scale)
    y_b = short_conv_branch(b_gate, c_gate, v, conv_w)
    m = jax.nn.sigmoid(ga) * y_a + jax.nn.sigmoid(gb) * y_b
    x = x + rmsnorm(jnp.einsum('bsd,de->bse', m, w_out), g_post_mix)
    f = rmsnorm(x, g_pre_ffn)
    hid = jax.nn.silu(jnp.einsum('bsd,df->bsf', f, w_ffn_gate)) * jnp.einsum('bsd,df->bsf', f, w_ffn_up)
    x = x + rmsnorm(jnp.einsum('bsf,fd->bsd', hid, w_ffn_down), g_post_ffn)
    return x


def setup_inputs(seed: int = 0) -> dict:
    key = jax.random.key(seed)
    ks = jax.random.split(key, 16)
    f32 = jnp.float32
    x_prompt = jax.random.normal(ks[0], (BATCH, SEQ, D_MODEL), f32)
    x_sample = jax.random.normal(ks[1], (DEC_BATCH, DEC_SEQ, D_MODEL), f32)
    w_in = jax.random.normal(ks[2], (DEPTH, D_MODEL, IN_COLS), f32) * D_MODEL ** -0.5
    pool_w = jax.random.normal(ks[3], (DEPTH, N_POOL_GROUPS, POOL_GROUP, POOL_GROUP), f32) * POOL_GROUP ** -0.5
    pool_scale = 1.0 + 0.1 * jax.random.normal(ks[4], (DEPTH, POOL_WIDTH), f32)
    conv_w = jax.random.normal(ks[5], (DEPTH, CONV_K, CONV_WIDTH), f32) * CONV_K ** -0.5
    w_out = jax.random.normal(ks[6], (DEPTH, D_MODEL, D_MODEL), f32) * D_MODEL ** -0.5
    w_ffn_gate = jax.random.normal(ks[7], (DEPTH, D_MODEL, FFN_HIDDEN), f32) * D_MODEL ** -0.5
    w_ffn_up = jax.random.normal(ks[8], (DEPTH, D_MODEL, FFN_HIDDEN), f32) * D_MODEL ** -0.5
    w_ffn_down = jax.random.normal(ks[9], (DEPTH, FFN_HIDDEN, D_MODEL), f32) * FFN_HIDDEN ** -0.5
    g_pre_mix = 1.0 + 0.05 * jax.random.normal(ks[10], (DEPTH, D_MODEL), f32)
    g_post_mix = 1.0 + 0.05 * jax.random.normal(ks[11], (DEPTH, D_MODEL), f32)
    g_pre_ffn = 1.0 + 0.05 * jax.random.normal(ks[12], (DEPTH, D_MODEL), f32)
    g_post_ffn = 1.0 + 0.05 * jax.random.normal(ks[13], (DEPTH, D_MODEL), f32)
    return {"x_prompt": x_prompt, "x_sample": x_sample, "w_in": w_in, "pool_w": pool_w,
            "pool_scale": pool_scale, "conv_w": conv_w, "w_out": w_out, "w_ffn_gate": w_ffn_gate,
            "w_ffn_up": w_ffn_up, "w_ffn_down": w_ffn_down, "g_pre_mix": g_pre_mix,
            "g_post_mix": g_post_mix, "g_pre_ffn": g_pre_ffn, "g_post_ffn": g_post_ffn}


def reference(x_prompt, x_sample, w_in, pool_w, pool_scale, conv_w, w_out, w_ffn_gate, w_ffn_up,
              w_ffn_down, g_pre_mix, g_post_mix, g_pre_ffn, g_post_ffn):
    y_prompt = x_prompt
    y_sample = x_sample
    for l in range(DEPTH):
        layer_params = (w_in[l], pool_w[l], pool_scale[l], conv_w[l], w_out[l], w_ffn_gate[l],
                        w_ffn_up[l], w_ffn_down[l], g_pre_mix[l], g_post_mix[l], g_pre_ffn[l],
                        g_post_ffn[l])
        y_prompt = encoder_layer(y_prompt, *layer_params)
        y_sample = encoder_layer(y_sample, *layer_params)
    return (y_prompt, y_sample)
```

```python
import contextlib
import numpy as np
import concourse.bass as bass
import concourse.mybir as mybir
from concourse.bass_utils import run_bass_kernel_spmd

F32 = mybir.dt.float32
BF16 = mybir.dt.bfloat16
AF = mybir.ActivationFunctionType
ALU = mybir.AluOpType

NCORES = 8
D = 1024
KC = 8
FF = 2816
JC = 22
DEPTH = 2
NT = 6
TV = 1024
HALO = 16
W = TV + 2 * HALO
NSUB = 3
SW = W // NSUB
NSLOT = 11
SLOTN = 2048
WIN = (2, 4, 8, 16)
EPS = 1e-6
NCONST = 128 + 2 * NT + NT * 4 * 16
SILU_LUT = False
SIGMOID_LUT = False
CH_MINQ = 3
CH_DIV = 3
CH_BDEF_MIX = 48
CH_BDEF = 16
NPOOL_LOW = 4
SEC_POOL, SEC_B, SEC_C, SEC_V, SEC_GA, SEC_GB = 0, 1, 2, 3, 4, 5
SEC_ORDER = (SEC_POOL, SEC_GA, SEC_V, SEC_GB, SEC_C, SEC_B)


class Buf:
    __slots__ = ("w", "rd")

    def __init__(self):
        self.w = None
        self.rd = {}


class Sched:
    ENG = ("pe", "act", "dve", "pool", "sp")

    def __init__(self):
        self.streams = {e: [] for e in self.ENG}
        self.cnt = {}
        self.known = {e: {} for e in self.ENG}

    def _deps(self, eng, reads, writes):
        deps = {}

        def add(key, val, kind):
            if key == eng and eng == "pe":
                return
            if deps.get(key, 0) < val:
                deps[key] = val

        for b in reads:
            if b.w is not None:
                add(b.w[0], b.w[1], "raw")
        for b in writes:
            if b.w is not None:
                add(b.w[0], b.w[1], "waw")
            for k, v in b.rd.items():
                add(k, v, "war")
        waits = []
        kn = self.known[eng]
        for key, val in deps.items():
            if kn.get(key, 0) >= val:
                continue
            kn[key] = val
            waits.append((key, val))
        return waits

    def _commit(self, tok, reads, writes):
        for b in reads:
            if b.rd.get(tok[0], 0) < tok[1]:
                b.rd[tok[0]] = tok[1]
        for b in writes:
            b.w = tok
            b.rd = {}

    def op(self, eng, fn, reads=(), writes=()):
        waits = self._deps(eng, reads, writes)
        self.cnt[eng] = self.cnt.get(eng, 0) + 1
        tok = (eng, self.cnt[eng])
        self.streams[eng].append((waits, fn, eng, 1))
        self._commit(tok, reads, writes)

    def dma(self, queue, fn, semkey, reads=(), writes=()):
        waits = self._deps(queue, reads, writes)
        self.cnt[semkey] = self.cnt.get(semkey, 0) + 16
        tok = (semkey, self.cnt[semkey])
        self.streams[queue].append((waits, fn, semkey, 16))
        self._commit(tok, reads, writes)

    def final_wait(self, queue, keys):
        waits = [(k, self.cnt[k]) for k in keys if self.cnt.get(k, 0) > 0]
        self.streams[queue].append((waits, None, None, 0))


def build_nc():
    nc = bass.Bass("TRN2", target_bir_lowering=False)
    xin = nc.dram_tensor("xin", [NT, 128, KC, W], F32, kind="ExternalInput").ap()
    cst = nc.dram_tensor("cst", [128, NCONST], F32, kind="ExternalInput").ap()
    win = nc.dram_tensor("win", [DEPTH, 24, 128, 2048], F32, kind="ExternalInput").ap()
    pwd = nc.dram_tensor("pwd", [DEPTH, 4, 128, 512], F32, kind="ExternalInput").ap()
    wod = nc.dram_tensor("wod", [DEPTH, 4, 128, 2048], F32, kind="ExternalInput").ap()
    wgu = nc.dram_tensor("wgu", [DEPTH, 22, 128, 2048], F32, kind="ExternalInput").ap()
    wdd = nc.dram_tensor("wdd", [DEPTH, 16, 128, 1408], F32, kind="ExternalInput").ap()
    yout = nc.dram_tensor("yout", [NT, 128, KC, TV], F32, kind="ExternalOutput").ap()

    with contextlib.ExitStack() as es:
        def sb(name, shape, dt):
            return es.enter_context(nc.sbuf_tensor(name, shape, dt))

        x_t = sb("x_t", [128, KC, W], F32)
        h_t = sb("h_t", [128, KC, W], BF16)
        o_t = sb("o_t", [128, KC, W], F32)
        m_t = sb("m_t", [128, KC, W], BF16)
        R = sb("R", [128, 12 * W], F32)
        slots = [sb(f"slot{i}", [128, SLOTN], BF16) for i in range(NSLOT)]
        lnv = [sb(f"lnv{i}", [128, SW], F32) for i in range(2)]
        NSG = 4
        sgb_ = [sb(f"sg{i}", [128, SW], F32) for i in range(NSG)]
        onef = sb("onef", [128, 1], F32)
        cs = sb("cs", [128, NCONST], F32)
        ones = sb("ones", [128, 128], BF16)
        epst = sb("epst", [128, 1], F32)
        psw = sb("psw", [128, DEPTH * KC], F32)
        banks = [es.enter_context(nc.psum_tensor(f"bank{i}", [128, 512], F32)) for i in range(8)]

        hidv = R[:, 0:11 * W].bitcast(BF16)

        def hid(j, s):
            return hidv[:, j * W + s * SW: j * W + (s + 1) * SW]

        def Fp(i, a=0, b=W):
            return R[:, i * W + a: i * W + b]

        def Ep(bi, hh, a=0, b=W):
            v = R[:, (10 + bi) * W:(11 + bi) * W].bitcast(BF16)
            return v[:, hh * W + a: hh * W + b]

        def col(s):
            return slice(s * SW, (s + 1) * SW)

        def cc(i):
            return cs[:, i:i + 1]


        def emit_all(S, plan, rec):
            X = [[Buf() for _ in range(NSUB)] for _ in range(KC)]
            H = [[Buf() for _ in range(NSUB)] for _ in range(KC)]
            O = [[Buf() for _ in range(NSUB)] for _ in range(KC)]
            M = [[Buf() for _ in range(NSUB)] for _ in range(KC)]
            FB = [[Buf() for _ in range(NSUB)] for _ in range(10)]
            EB = [[Buf() for _ in range(2)] for _ in range(2)]
            HID = [[Buf() for _ in range(NSUB)] for _ in range(JC)]
            SLOTB = [Buf() for _ in range(NSLOT)]
            BANKB = [Buf() for _ in range(8)]
            LNB = [Buf() for _ in range(2)]
            SGB = [Buf() for _ in range(NSG)]
            CSB = Buf()
            MISC = Buf()
            YB = [Buf() for _ in range(NSUB)]

            state = {"bank": 0, "unit": 0, "ln": 0, "sg": 0, "issued": 0, "released": 0}
            pend = []

            def next_bank():
                b = state["bank"]
                state["bank"] = (b + 1) % 8
                return b

            def issue_upto(n):
                while state["issued"] <= n and state["issued"] < len(plan):
                    u = state["issued"]
                    src_ap, nel = plan[u]
                    si = u % NSLOT
                    S.dma("pool", (lambda g, si=si, src_ap=src_ap, nel=nel:
                                   g.dma_start(out=slots[si][:, 0:nel], in_=src_ap)),
                          ("slot", si), reads=(), writes=(SLOTB[si],))
                    state["issued"] = u + 1

            def load_unit(src_ap, nel):
                u = state["unit"]
                state["unit"] = u + 1
                si = u % NSLOT
                if plan is None:
                    rec.append((src_ap, nel))
                    S.dma("pool", (lambda g, si=si, src_ap=src_ap, nel=nel:
                                   g.dma_start(out=slots[si][:, 0:nel], in_=src_ap)),
                          ("slot", si), reads=(), writes=(SLOTB[si],))
                else:
                    issue_upto(max(u, NSLOT - 1))
                return si, u

            def release(u):
                state["released"] = max(state["released"], u + 1)
                if plan is not None:
                    issue_upto(u + NSLOT)

            def defer(nmm, steps, tag=None):
                if callable(steps):
                    steps = [steps]
                pend.append([nmm, list(steps), tag])

            pend2 = []

            def defer2(nmm, fn, tag):
                pend2.append([nmm, fn, tag])

            def flush2(tag=None):
                for it in [it for it in pend2 if tag is None or it[2] == tag]:
                    pend2.remove(it)
                    it[1]()

            def drain_front():
                it = pend[0]
                while it[1]:
                    it[1].pop(0)()
                pend.pop(0)

            def ensure_h(s):
                while any(it[2] == s for it in pend):
                    drain_front()

            def run_pending(nmm):
                quota = max(CH_MINQ, nmm // CH_DIV)
                while pend and pend[0][0] <= 0 and quota > 0:
                    it = pend[0]
                    while it[1] and quota > 0:
                        it[1].pop(0)()
                        quota -= 1
                    if not it[1]:
                        pend.pop(0)
                for it in pend:
                    it[0] -= nmm
                for it in [it for it in pend2 if it[0] <= 0]:
                    pend2.remove(it)
                    it[1]()
                for it in pend2:
                    it[0] -= nmm

            def mm_group(bi, pairs, reads, main=True):
                if main:
                    run_pending(len(pairs))
                out_ap = banks[bi][:, 0:SW]

                def fn(pe, pairs=pairs, out_ap=out_ap):
                    n = len(pairs)
                    ins = None
                    for i, (l_, r_) in enumerate(pairs):
                        ins = pe.matmul(out_ap, lhsT=l_, rhs=r_, start=(i == 0), stop=(i == n - 1))
                    return ins
                S.op("pe", fn, reads=reads, writes=(BANKB[bi],))

            def act(out, in_, func, reads, writes, bias=None, scale=None):
                kw = {}
                if bias is not None:
                    kw["bias"] = bias
                if scale is not None:
                    kw["scale"] = scale
                S.op("act", (lambda a, out=out, in_=in_, func=func, kw=kw:
                             a.activation(out=out, in_=in_, func=func, **kw)), reads=reads, writes=writes)

            def tt(out, in0, in1, op, reads, writes, eng="dve"):
                S.op(eng, (lambda v, out=out, in0=in0, in1=in1, op=op:
                           v.tensor_tensor(out=out, in0=in0, in1=in1, op=op)), reads=reads, writes=writes)

            def stt(out, in0, scalar, in1, op0, op1, reads, writes):
                S.op("dve", (lambda v, out=out, in0=in0, scalar=scalar, in1=in1, op0=op0, op1=op1:
                             v.scalar_tensor_tensor(out=out, in0=in0, scalar=scalar, in1=in1, op0=op0, op1=op1)),
                     reads=reads, writes=writes)

            def tsmul(out, in0, scalar, reads, writes):
                S.op("dve", (lambda v, out=out, in0=in0, scalar=scalar:
                             v.tensor_scalar_mul(out=out, in0=in0, scalar1=scalar)), reads=reads, writes=writes)

            S.dma("sp", lambda q: q.dma_start(out=cs[:], in_=cst[:]), ("cst", 0), writes=(CSB,))
            S.op("dve", lambda v: v.memset(ones[:], 1.0), writes=(MISC,))
            S.op("dve", lambda v: v.memset(epst[:], EPS), writes=(MISC,))
            S.op("dve", lambda v: v.memset(onef[:], 1.0), writes=(MISC,))
            S.op("dve", lambda v: v.memset(R[:], 0.0),
                 writes=[b for p in FB for b in p] + [b for p in EB for b in p])
            S.op("dve", lambda v: v.memset(m_t[:], 0.0), writes=[b for p in M for b in p])
            S.op("dve", lambda v: v.memset(h_t[:], 0.0), writes=[b for p in H for b in p])
            for l in range(DEPTH):
                for g in range(4):
                    S.op("dve", (lambda v, l=l, g=g: v.tensor_scalar_mul(
                        out=psw[:, l * KC + 2 * g: l * KC + 2 * g + 2],
                        in0=cs[:, l * 64 + 32 + 2 * g: l * 64 + 32 + 2 * g + 2],
                        scalar1=1.0 / WIN[g])), reads=(CSB,), writes=(MISC,))

            def sigmoid_to(dst, bank_ap, bankbuf, dstbufs):
                if SIGMOID_LUT:
                    act(dst, bank_ap, AF.Sigmoid, reads=(bankbuf,), writes=dstbufs)
                    return
                gi = state["sg"] % NSG
                state["sg"] += 1
                tmp = sgb_[gi][:, :]
                act(tmp, bank_ap, AF.Exp, reads=(bankbuf,), writes=(SGB[gi],), scale=-1.0)
                act(tmp, tmp, AF.Ln, reads=(SGB[gi], MISC), writes=(SGB[gi],), bias=onef[:, 0:1])
                act(dst, tmp, AF.Exp, reads=(SGB[gi],), writes=dstbufs, scale=-1.0)

            def rstd_for(s, sqsrc, SQB):
                bi = next_bank()
                pairs = [(ones[:, :], sqsrc(k, s)) for k in range(KC)]
                mm_group(bi, pairs, reads=[SQB[k][s] for k in range(KC)] + [MISC], main=False)
                li = state["ln"] % 2
                state["ln"] += 1
                act(lnv[li][:, :], banks[bi][:, 0:SW], AF.Ln, reads=(BANKB[bi], MISC), writes=(LNB[li],),
                    bias=epst[:, 0:1], scale=1.0 / D)
                act(banks[bi][:, 0:SW], lnv[li][:, :], AF.Exp, reads=(LNB[li],), writes=(BANKB[bi],), scale=-0.5)
                return bi

            def load_x(t, s):
                S.dma("sp", (lambda q, t=t, s=s: q.dma_start(out=x_t[:, :, col(s)],
                                                             in_=xin[t, :, :, s * SW:(s + 1) * SW])),
                      ("xl", s), writes=[X[k][s] for k in range(KC)])

            def halo_mask(t, s):
                if s == 0:
                    bl = [X[k][0] for k in range(KC)]
                    tsmul(x_t[:, :, 0:HALO], x_t[:, :, 0:HALO], cc(128 + 2 * t), reads=bl + [CSB], writes=bl)
                if s == NSUB - 1:
                    bl = [X[k][NSUB - 1] for k in range(KC)]
                    tsmul(x_t[:, :, W - HALO:W], x_t[:, :, W - HALO:W], cc(128 + 2 * t + 1),
                          reads=bl + [CSB], writes=bl)

            def pre_sq(s):
                for k in list(range(KC // 2, KC)) + list(range(KC // 2)):
                    act(m_t[:, k, col(s)], x_t[:, k, col(s)], AF.Square, reads=(X[k][s],), writes=(M[k][s],))

            def chain_B(s, gbase, after=None):
                cell = {}
                steps = [lambda: cell.__setitem__("bi", rstd_for(s, lambda k, s_: m_t[:, k, col(s_)], M))]
                for k in range(KC):
                    steps.append(lambda k=k: stt(
                        h_t[:, k, col(s)], x_t[:, k, col(s)], cc(gbase + k), banks[cell["bi"]][:, 0:SW],
                        ALU.mult, ALU.mult, reads=(X[k][s], BANKB[cell["bi"]], CSB), writes=(H[k][s],)))
                if after is not None:
                    steps.append(after)
                return steps

            def after_mid(s, t):
                lo = max(s * SW, HALO)
                hi = min((s + 1) * SW, W - HALO)
                S.dma("sp", (lambda q, t=t, lo=lo, hi=hi: q.dma_start(
                    out=yout[t, :, :, lo - HALO:hi - HALO], in_=x_t[:, :, lo:hi])),
                      ("st", s), reads=[X[k][s] for k in range(KC)], writes=(YB[s],))
                if t + 1 < NT:
                    load_x(t + 1, s)

                    def nxt(s=s, t=t):
                        halo_mask(t + 1, s)
                        pre_sq(s)
                    defer2(220, nxt, s)

            def chain_A(s, t, l, which):
                cb = l * 64
                cell = {}
                final = which == "ffn" and l == DEPTH - 1
                lo = max(s * SW, HALO)
                hi = min((s + 1) * SW, W - HALO)
                gpost = cb + (8 if which == "mix" else 24)
                steps = [lambda: cell.__setitem__("bi", rstd_for(s, lambda k, s_: h_t[:, k, col(s_)], H))]

                def mult_r(k):
                    bi = cell["bi"]
                    stt(o_t[:, k, col(s)], o_t[:, k, col(s)], cc(gpost + k), banks[bi][:, 0:SW], ALU.mult, ALU.mult,
                        reads=(O[k][s], BANKB[bi], CSB), writes=(O[k][s],))

                def add_x(k, eng):
                    tt(x_t[:, k, col(s)], x_t[:, k, col(s)], o_t[:, k, col(s)], ALU.add,
                       reads=(X[k][s], O[k][s]), writes=(X[k][s],), eng=eng)

                if final:
                    def fin(k):
                        mult_r(k)
                        S.dma("pool", (lambda g_, t=t, k=k, lo=lo, hi=hi: g_.dma_start(
                            out=yout[t, :, k, lo - HALO:hi - HALO], in_=o_t[:, k, lo:hi], accum_op=ALU.add)),
                              ("acc", k * NSUB + s), reads=[O[k][s], YB[s]], writes=())
                    for k in range(KC):
                        steps.append(lambda k=k: fin(k))
                    if t + 1 < NT:
                        def to_next():
                            flush2(s)
                            defer(24, chain_B(s, 0), tag=s)
                        steps.append(to_next)
                    return steps

                korder = list(range(KC // 2, KC)) + list(range(KC // 2))

                def ma(k):
                    mult_r(k)
                    add_x(k, "pool" if (k >= KC // 2 or k < NPOOL_LOW) else "dve")
                for k in korder:
                    steps.append(lambda k=k: ma(k))
                if which == "mix":
                    gnext = cb + 16
                    aft = (lambda: after_mid(s, t)) if l == DEPTH - 1 else None
                else:
                    gnext = (l + 1) * 64
                    aft = None
                    steps.append(lambda: halo_mask(t, s))
                for k in korder:
                    steps.append(lambda k=k: act(m_t[:, k, col(s)], x_t[:, k, col(s)], AF.Square,
                                                 reads=(X[k][s],), writes=(M[k][s],)))
                steps.append(lambda: defer(CH_BDEF_MIX if which == "mix" else CH_BDEF, chain_B(s, gnext, aft), tag=s))
                return steps

            def out_proj_evac(bi, c, s, gbase):
                S.op("dve", (lambda v, c=c, s=s, bi=bi: v.tensor_copy(out=o_t[:, c, col(s)], in_=banks[bi][:, 0:SW])),
                     reads=(BANKB[bi],), writes=(O[c][s],))
                act(h_t[:, c, col(s)], o_t[:, c, col(s)], AF.Square, reads=(O[c][s],), writes=(H[c][s],))

            def proj_group(si, hh, src, s, SRC):
                if SRC is H:
                    ensure_h(s)
                bi = next_bank()
                pairs = [(slots[si][:, k * 256 + hh * 128: k * 256 + hh * 128 + 128], src[:, k, col(s)])
                         for k in range(KC)]
                mm_group(bi, pairs, reads=[SLOTB[si]] + [SRC[k][s] for k in range(KC)])
                return bi

            for s in range(NSUB):
                load_x(0, s)
            for s in range(NSUB):
                halo_mask(0, s)
                pre_sq(s)
                for st_ in chain_B(s, 0):
                    st_()

            for t in range(NT):
                for l in range(DEPTH):
                    cb = l * 64
                    for g in range(4):
                        wdw = WIN[g]
                        eb = g % 2
                        UP = (0, 1)
                        TA, TB = 2, 3
                        SGA = (4, 5)
                        VV = (6, 7)
                        SGBp = (8, 9)
                        corr_base = 128 + 2 * NT + (t * 4 + g) * 16

                        def ev_pool(bi, hh, s):
                            act(Fp(UP[hh], s * SW, (s + 1) * SW), banks[bi][:, 0:SW], AF.Copy,
                                reads=(BANKB[bi],), writes=(FB[UP[hh]][s],))

                        def ev_ga(bi, hh, s):
                            sigmoid_to(Fp(SGA[hh], s * SW, (s + 1) * SW), banks[bi][:, 0:SW], BANKB[bi],
                                       (FB[SGA[hh]][s],))

                        def ev_v(bi, hh, s):
                            act(Fp(VV[hh], s * SW, (s + 1) * SW), banks[bi][:, 0:SW], AF.Copy,
                                reads=(BANKB[bi],), writes=(FB[VV[hh]][s],))

                        def ev_gb(bi, hh, s):
                            sigmoid_to(Fp(SGBp[hh], s * SW, (s + 1) * SW), banks[bi][:, 0:SW], BANKB[bi],
                                       (FB[SGBp[hh]][s],))

                        def ev_c(bi, hh, s):
                            tt(Fp(VV[hh], s * SW, (s + 1) * SW), banks[bi][:, 0:SW], Fp(VV[hh], s * SW, (s + 1) * SW),
                               ALU.mult, reads=(BANKB[bi], FB[VV[hh]][s]), writes=(FB[VV[hh]][s],))

                        def ev_b(bi, hh, s):
                            tt(Fp(SGBp[hh], s * SW, (s + 1) * SW), banks[bi][:, 0:SW],
                               Fp(SGBp[hh], s * SW, (s + 1) * SW),
                               ALU.mult, reads=(BANKB[bi], FB[SGBp[hh]][s]), writes=(FB[SGBp[hh]][s],))

                        def unit_of(sec):
                            return load_unit(win[l, g * 6 + SEC_ORDER.index(sec)], 2048)

                        def section(sec, evac):
                            si, u = unit_of(sec)
                            for hh in range(2):
                                for s in range(NSUB):
                                    bi = proj_group(si, hh, h_t, s, H)
                                    evac(bi, hh, s)
                            release(u)

                        def window_ops():
                            for hh in range(2):
                                up = UP[hh]
                                tt(Fp(TA, 1, W), Fp(up, 0, W - 1), Fp(up, 1, W), ALU.add,
                                   reads=FB[up], writes=FB[TA])
                                cur, lo, hi = TA, 1, W
                                if wdw >= 4:
                                    tt(Fp(TB, 2, W - 1), Fp(TA, 1, W - 2), Fp(TA, 3, W), ALU.add,
                                       reads=FB[TA], writes=FB[TB])
                                    cur, lo, hi = TB, 2, W - 1
                                if wdw >= 8:
                                    tt(Fp(TA, 4, W - 3), Fp(TB, 2, W - 5), Fp(TB, 6, W - 1), ALU.add,
                                       reads=FB[TB], writes=FB[TA])
                                    cur, lo, hi = TA, 4, W - 3
                                if wdw >= 16:
                                    tt(Fp(TB, 8, W - 7), Fp(TA, 4, W - 11), Fp(TA, 12, W - 3), ALU.add,
                                       reads=FB[TA], writes=FB[TB])
                                    cur, lo, hi = TB, 8, W - 7
                                tt(Fp(cur, HALO, HALO + 8), Fp(cur, HALO, HALO + 8),
                                   cs[:, corr_base:corr_base + 8], ALU.mult,
                                   reads=[FB[cur][0], CSB], writes=[FB[cur][0]])
                                tt(Fp(cur, W - HALO - 8, W - HALO), Fp(cur, W - HALO - 8, W - HALO),
                                   cs[:, corr_base + 8:corr_base + 16], ALU.mult,
                                   reads=[FB[cur][NSUB - 1], CSB], writes=[FB[cur][NSUB - 1]])
                                stt(Ep(eb, hh, lo, hi), Fp(up, lo, hi), float(-wdw), Fp(cur, lo, hi),
                                    ALU.mult, ALU.add, reads=FB[up] + FB[cur], writes=(EB[eb][hh],))

                        if g == 0:
                            blk = [(unit_of(SEC_POOL), ev_pool), (unit_of(SEC_GA), ev_ga), (unit_of(SEC_V), ev_v),
                                   (unit_of(SEC_GB), ev_gb)]
                            for s in range(NSUB):
                                for (si, u), ev in blk:
                                    for hh in range(2):
                                        bi = proj_group(si, hh, h_t, s, H)
                                        ev(bi, hh, s)
                            for (si, u), ev in blk:
                                release(u)
                            window_ops()
                        else:
                            section(SEC_POOL, ev_pool)
                            window_ops()
                            section(SEC_GA, ev_ga)
                            section(SEC_V, ev_v)
                        if g != 0:
                            section(SEC_GB, ev_gb)
                        section(SEC_C, ev_c)

                        def conv_chain(hh):
                            c = 2 * g + hh
                            yb, cv = UP[hh], VV[hh]
                            act(Fp(yb), Fp(cv), AF.Copy, reads=FB[cv] + [CSB], writes=FB[yb],
                                scale=cc(cb + 48 + c))
                            stt(Fp(yb, 1, W), Fp(cv, 0, W - 1), cc(cb + 40 + c), Fp(yb, 1, W), ALU.mult, ALU.add,
                                reads=FB[cv] + FB[yb] + [CSB], writes=FB[yb])
                            stt(Fp(yb, 0, W - 1), Fp(cv, 1, W), cc(cb + 56 + c), Fp(yb, 0, W - 1), ALU.mult, ALU.add,
                                reads=FB[cv] + FB[yb] + [CSB], writes=FB[yb])

                        conv_chain(0)
                        sp_, up_u = load_unit(pwd[l, g], 512)
                        for hh in range(2):
                            c = 2 * g + hh
                            for s in range(NSUB):
                                bi = next_bank()
                                pairs = [(slots[sp_][:, k2 * 256 + hh * 128: k2 * 256 + hh * 128 + 128],
                                          Ep(eb, k2, s * SW, (s + 1) * SW)) for k2 in range(2)]
                                mm_group(bi, pairs, reads=[SLOTB[sp_], EB[eb][0], EB[eb][1]])
                                stt(Fp(SGA[hh], s * SW, (s + 1) * SW), banks[bi][:, 0:SW],
                                    psw[:, l * KC + c: l * KC + c + 1], Fp(SGA[hh], s * SW, (s + 1) * SW),
                                    ALU.mult, ALU.mult,
                                    reads=(BANKB[bi], FB[SGA[hh]][s], MISC), writes=(FB[SGA[hh]][s],))
                        release(up_u)
                        conv_chain(1)
                        if g == 3:
                            si, u = unit_of(SEC_B)
                            for s in range(NSUB):
                                for hh in range(2):
                                    bi = proj_group(si, hh, h_t, s, H)
                                    ev_b(bi, hh, s)
                                for hh in range(2):
                                    c = 2 * g + hh
                                    a_, b_ = s * SW, (s + 1) * SW
                                    tt(Fp(SGBp[hh], a_, b_), Fp(SGBp[hh], a_, b_), Fp(UP[hh], a_, b_), ALU.mult,
                                       reads=(FB[SGBp[hh]][s], FB[UP[hh]][s]), writes=(FB[SGBp[hh]][s],))
                                    tt(m_t[:, c, col(s)], Fp(SGA[hh], a_, b_), Fp(SGBp[hh], a_, b_), ALU.add,
                                       reads=(FB[SGA[hh]][s], FB[SGBp[hh]][s]), writes=(M[c][s],))
                            release(u)
                        else:
                            section(SEC_B, ev_b)
                        for hh in range(2 if g != 3 else 0):
                            c = 2 * g + hh
                            tt(Fp(SGBp[hh]), Fp(SGBp[hh]), Fp(UP[hh]), ALU.mult,
                               reads=FB[SGBp[hh]] + FB[UP[hh]], writes=FB[SGBp[hh]])
                            tt(m_t[:, c, :], Fp(SGA[hh]), Fp(SGBp[hh]), ALU.add,
                               reads=FB[SGA[hh]] + FB[SGBp[hh]], writes=M[c])

                    wo_u = [load_unit(wod[l, u], 2048) for u in range(4)]
                    for s in range(NSUB):
                        for c in range(KC):
                            si, _ = wo_u[c // 2]
                            bi = proj_group(si, c % 2, m_t, s, M)
                            out_proj_evac(bi, c, s, cb + 8)
                        defer(16, chain_A(s, t, l, "mix"), tag=s)
                    for si, u in wo_u:
                        release(u)

                    def gu_step(sg_, su_, hh, j, s):
                        bg = proj_group(sg_, hh, h_t, s, H)
                        bu = proj_group(su_, hh, h_t, s, H)
                        gi = state["sg"] % NSG
                        state["sg"] += 1
                        tmp = sgb_[gi][:, :]
                        if SILU_LUT:
                            act(tmp, banks[bg][:, 0:SW], AF.Silu, reads=(BANKB[bg],), writes=(SGB[gi],))
                        else:
                            act(tmp, banks[bg][:, 0:SW], AF.Exp, reads=(BANKB[bg],), writes=(SGB[gi],), scale=-1.0)
                            act(tmp, tmp, AF.Ln, reads=(SGB[gi], MISC), writes=(SGB[gi],), bias=onef[:, 0:1])
                            act(tmp, tmp, AF.Exp, reads=(SGB[gi],), writes=(SGB[gi],), scale=-1.0)
                            tt(tmp, banks[bg][:, 0:SW], tmp, ALU.mult, reads=(BANKB[bg], SGB[gi]), writes=(SGB[gi],))
                        tt(hid(j, s), banks[bu][:, 0:SW], tmp, ALU.mult,
                           reads=(BANKB[bu], SGB[gi]), writes=(HID[j][s],))

                    NB0 = 4
                    blk = [(load_unit(wgu[l, 2 * jj], 2048), load_unit(wgu[l, 2 * jj + 1], 2048)) for jj in range(NB0)]
                    for s in range(NSUB):
                        for jj in range(NB0):
                            for hh in range(2):
                                gu_step(blk[jj][0][0], blk[jj][1][0], hh, 2 * jj + hh, s)
                    for (sg_, ug), (su_, uu) in blk:
                        release(ug)
                        release(uu)
                    for jj in range(NB0, 11):
                        sg_, ug = load_unit(wgu[l, 2 * jj], 2048)
                        su_, uu = load_unit(wgu[l, 2 * jj + 1], 2048)
                        for hh in range(2):
                            for s in range(NSUB):
                                gu_step(sg_, su_, hh, 2 * jj + hh, s)
                        release(ug)
                        release(uu)

                    def down_group(s0, s1, c, s):
                        bi = next_bank()
                        pairs = []
                        for j in range(JC):
                            sl = s0 if j < 11 else s1
                            jo = j % 11
                            pairs.append((slots[sl][:, jo * 128: jo * 128 + 128], hid(j, s)))
                        mm_group(bi, pairs, reads=[SLOTB[s0], SLOTB[s1]] + [HID[j][s] for j in range(JC)])
                        out_proj_evac(bi, c, s, cb + 24)

                    NTAIL = 4
                    for c in range(KC - NTAIL):
                        s0, u0 = load_unit(wdd[l, 2 * c], 1408)
                        s1, u1 = load_unit(wdd[l, 2 * c + 1], 1408)
                        for s in range(NSUB):
                            down_group(s0, s1, c, s)
                        release(u0)
                        release(u1)
                    tl = [(load_unit(wdd[l, 2 * c], 1408), load_unit(wdd[l, 2 * c + 1], 1408))
                          for c in range(KC - NTAIL, KC)]
                    for s in range(NSUB):
                        for i, c in enumerate(range(KC - NTAIL, KC)):
                            down_group(tl[i][0][0], tl[i][1][0], c, s)
                        defer(16, chain_A(s, t, l, "ffn"), tag=s)

                    def rel_tail(tl=tl):
                        for (s0, u0), (s1, u1) in tl:
                            release(u0)
                            release(u1)
                    defer(40, rel_tail)

            while pend:
                drain_front()
            flush2()
            S.final_wait("sp", [("st", i) for i in range(NSUB)] + [("acc", i) for i in range(KC * NSUB)])

        rec = []
        emit_all(Sched(), None, rec)
        S = Sched()
        emit_all(S, rec, None)

        keys = set(S.cnt.keys())
        sems = {}
        for i, key in enumerate(sorted(keys, key=str)):
            sems[key] = es.enter_context(nc.semaphore(f"sem{i}"))
        block = es.enter_context(nc.Block())

        def run_stream(engine, name):
            for waits, fn, ikey, iamt in S.streams[name]:
                for key, val in waits:
                    engine.wait_ge(sems[key], val)
                if fn is not None:
                    ins = fn(engine)
                    ins.then_inc(sems[ikey], iamt)

        @block.sync
        def _(e):
            run_stream(e, "sp")

        @block.gpsimd
        def _(e):
            run_stream(e, "pool")

        @block.tensor
        def _(e):
            run_stream(e, "pe")

        @block.scalar
        def _(e):
            run_stream(e, "act")

        @block.vector
        def _(e):
            run_stream(e, "dve")
    return nc


def _tile_specs(core):
    specs = []
    for t in range(4):
        specs.append(("s", 2 * core + t // 2, (t % 2) * TV, 2048))
    for t in range(2):
        specs.append(("p", core // 4, (core % 4) * 2048 + t * TV, 8192))
    return specs


def _fm(a):
    kk = a.shape[0] // 128
    return np.ascontiguousarray(a.reshape(kk, 128, a.shape[1]).transpose(1, 0, 2)).reshape(128, kk * a.shape[1])


def _vec(v):
    return np.ascontiguousarray(v.reshape(KC, 128).T)


def _prep_weights(w_in, pool_w, w_out, w_ffn_gate, w_ffn_up, w_ffn_down):
    win = np.empty((DEPTH, 24, 128, 2048), np.float32)
    pwd = np.empty((DEPTH, 4, 128, 512), np.float32)
    wod = np.empty((DEPTH, 4, 128, 2048), np.float32)
    wgu = np.empty((DEPTH, 22, 128, 2048), np.float32)
    wdd = np.empty((DEPTH, 16, 128, 1408), np.float32)
    for l in range(DEPTH):
        for g in range(4):
            for i, sec in enumerate(SEC_ORDER):
                c0 = sec * 1024 + g * 256
                win[l, g * 6 + i] = _fm(w_in[l][:, c0:c0 + 256])
            pwd[l, g] = _fm(pool_w[l][g])
            wod[l, g] = _fm(w_out[l][:, g * 256:(g + 1) * 256])
        for jj in range(11):
            wgu[l, 2 * jj] = _fm(w_ffn_gate[l][:, jj * 256:(jj + 1) * 256])
            wgu[l, 2 * jj + 1] = _fm(w_ffn_up[l][:, jj * 256:(jj + 1) * 256])
        for c in range(KC):
            for hf in range(2):
                wdd[l, 2 * c + hf] = _fm(w_ffn_down[l][hf * 1408:(hf + 1) * 1408, c * 128:(c + 1) * 128])
    return win, pwd, wod, wgu, wdd


def _prep_consts(core, pool_scale, conv_w, g_pre_mix, g_post_mix, g_pre_ffn, g_post_ffn):
    cst = np.zeros((128, NCONST), np.float32)
    for l in range(DEPTH):
        b = l * 64
        cst[:, b + 0:b + 8] = _vec(g_pre_mix[l])
        cst[:, b + 8:b + 16] = _vec(g_post_mix[l])
        cst[:, b + 16:b + 24] = _vec(g_pre_ffn[l])
        cst[:, b + 24:b + 32] = _vec(g_post_ffn[l])
        cst[:, b + 32:b + 40] = _vec(pool_scale[l])
        for tap in range(3):
            cst[:, b + 40 + 8 * tap:b + 48 + 8 * tap] = _vec(conv_w[l][tap])
    for t, (_, _, start, slen) in enumerate(_tile_specs(core)):
        left_edge = start == 0
        right_edge = start + TV == slen
        cst[:, 128 + 2 * t] = 0.0 if left_edge else 1.0
        cst[:, 128 + 2 * t + 1] = 0.0 if right_edge else 1.0
        for g, wdw in enumerate(WIN):
            half = wdw // 2
            base = 128 + 2 * NT + (t * 4 + g) * 16
            corr = np.ones(16, np.float32)
            if left_edge:
                for p in range(8):
                    cnt = min(p + half, slen) - max(p - half, 0)
                    corr[p] = wdw / cnt
            if right_edge:
                for i in range(8):
                    p = slen - 8 + i
                    cnt = min(p + half, slen) - max(p - half, 0)
                    corr[8 + i] = wdw / cnt
            cst[:, base:base + 16] = corr[None, :]
    return cst


def kernel(x_prompt, x_sample, w_in, pool_w, pool_scale, conv_w, w_out, w_ffn_gate, w_ffn_up,
           w_ffn_down, g_pre_mix, g_post_mix, g_pre_ffn, g_post_ffn):
    f = lambda a: np.asarray(a, dtype=np.float32)
    x_prompt, x_sample = f(x_prompt), f(x_sample)
    win, pwd, wod, wgu, wdd = _prep_weights(f(w_in), f(pool_w), f(w_out), f(w_ffn_gate), f(w_ffn_up),
                                            f(w_ffn_down))
    xp_pad = np.pad(x_prompt, ((0, 0), (HALO, HALO), (0, 0)))
    xs_pad = np.pad(x_sample, ((0, 0), (HALO, HALO), (0, 0)))
    in_maps = []
    for core in range(NCORES):
        xin = np.empty((NT, 128, KC, W), np.float32)
        for t, (which, si, start, _) in enumerate(_tile_specs(core)):
            src = xs_pad if which == "s" else xp_pad
            blk = src[si, start:start + W, :]
            xin[t] = blk.T.reshape(KC, 128, W).transpose(1, 0, 2)
        cst = _prep_consts(core, f(pool_scale), f(conv_w), f(g_pre_mix), f(g_post_mix), f(g_pre_ffn),
                           f(g_post_ffn))
        in_maps.append({"xin": xin, "cst": cst, "win": win, "pwd": pwd, "wod": wod, "wgu": wgu, "wdd": wdd})

    nc = build_nc()
    res = run_bass_kernel_spmd(nc, in_maps, core_ids=list(range(NCORES)))

    y_prompt = np.empty_like(x_prompt)
    y_sample = np.empty_like(x_sample)
    for core in range(NCORES):
        yo = res.results[core]["yout"]
        for t, (which, si, start, _) in enumerate(_tile_specs(core)):
            blk = yo[t].transpose(1, 0, 2).reshape(D, TV).T
            if which == "s":
                y_sample[si, start:start + TV, :] = blk
            else:
                y_prompt[si, start:start + TV, :] = blk
    return (y_prompt, y_sample)
```
